# Optimizing a Trainium2 kernel written in Bass

```python
import jax, jax.numpy as jnp
from jax import lax
import numpy as np

D_MODEL = 2048
BATCH = 8
SEQ = 4096
DEPTH = 2

CHUNK = 64
N_HEADS = 16
HEAD_DIM = D_MODEL // N_HEADS
D_FF = 4 * D_MODEL
Q_BLOCK = 128
N_PREV_CHUNKS = 8
REL_CLIP = 256
N_REL = REL_CLIP + CHUNK
N_A = DEPTH // 2
N_B = DEPTH - N_A
EPS = 1e-6
FGATE_BIAS = 3.0

kernel_name = "fox_yoco_chunked_relbias_hybrid"


def rms_norm(x, g):
    xf = x.astype(jnp.float32)
    y = xf * lax.rsqrt(jnp.mean(xf * xf, axis=-1, keepdims=True) + EPS)
    return (y * g.astype(jnp.float32)).astype(x.dtype)


def sq_relu_mlp(h, g, w1, w2):
    a = jax.nn.relu(rms_norm(h, g) @ w1)
    return (a * a) @ w2


def forgetting_attention(q, k, v, logf):
    S = q.shape[1]
    scale = HEAD_DIM ** -0.5
    c = jnp.transpose(jnp.cumsum(logf, axis=1), (0, 2, 1))
    outs = []
    for i in range(S // Q_BLOCK):
        q0, q1 = i * Q_BLOCK, (i + 1) * Q_BLOCK
        qb = q[:, q0:q1]
        kb, vb = k[:, :q1], v[:, :q1]
        s = jnp.einsum('bqhd,bkhd->bhqk', qb, kb).astype(jnp.float32) * scale
        s = s + c[:, :, q0:q1, None] - c[:, :, None, :q1]
        causal = (q0 + jnp.arange(Q_BLOCK))[:, None] >= jnp.arange(q1)[None, :]
        s = jnp.where(causal[None, None], s, -jnp.inf)
        p = jax.nn.softmax(s, axis=-1).astype(vb.dtype)
        outs.append(jnp.einsum('bhqk,bkhd->bqhd', p, vb))
    return jnp.concatenate(outs, axis=1)


def chunked_relbias_attention(q, k, v, rel_table):
    B, S, H, Dh = q.shape
    n_chunks = S // CHUNK
    pad = N_PREV_CHUNKS * CHUNK
    band = pad + CHUNK
    scale = HEAD_DIM ** -0.5
    kp = jnp.pad(k, ((0, 0), (pad, 0), (0, 0), (0, 0)))
    vp = jnp.pad(v, ((0, 0), (pad, 0), (0, 0), (0, 0)))
    qi = jnp.arange(CHUNK)[:, None]
    km = jnp.arange(band)[None, :]
    dist = pad + qi - km
    idx = jnp.clip(dist, -(CHUNK - 1), REL_CLIP) + (CHUNK - 1)
    bias = rel_table[:, idx].astype(jnp.float32)
    qc = q.reshape(B, n_chunks, CHUNK, H, Dh)

    def one_chunk(ci):
        qb = lax.dynamic_index_in_dim(qc, ci, axis=1, keepdims=False)
        kb = lax.dynamic_slice_in_dim(kp, ci * CHUNK, band, axis=1)
        vb = lax.dynamic_slice_in_dim(vp, ci * CHUNK, band, axis=1)
        s = jnp.einsum('bqhd,bkhd->bhqk', qb, kb).astype(jnp.float32) * scale + bias[None]
        valid = km >= pad - ci * CHUNK
        s = jnp.where(valid[None, None], s, -jnp.inf)
        p = jax.nn.softmax(s, axis=-1).astype(vb.dtype)
        return jnp.einsum('bhqk,bkhd->bqhd', p, vb)

    out = lax.map(one_chunk, jnp.arange(n_chunks))
    return jnp.transpose(out, (1, 0, 2, 3, 4)).reshape(B, S, H, Dh)


def setup_inputs(seed: int = 0) -> dict:
    key = jax.random.key(seed)
    ks = jax.random.split(key, 20)
    D, H, Dh = D_MODEL, N_HEADS, HEAD_DIM
    nrm = jax.random.normal
    f32 = jnp.float32
    return {
        "x": nrm(ks[0], (BATCH, SEQ, D), f32),
        "a_norm_g": 1.0 + 0.02 * nrm(ks[1], (N_A, D), f32),
        "a_w_in": nrm(ks[2], (N_A, D, 3 * D + H), f32) * D ** -0.5,
        "a_b_f": FGATE_BIAS + 0.5 * nrm(ks[3], (N_A, H), f32),
        "a_q_g": 1.0 + 0.02 * nrm(ks[4], (N_A, Dh), f32),
        "a_k_g": 1.0 + 0.02 * nrm(ks[5], (N_A, Dh), f32),
        "a_w_out": nrm(ks[6], (N_A, D, D), f32) * D ** -0.5,
        "mlp_norm_g": 1.0 + 0.02 * nrm(ks[7], (DEPTH, D), f32),
        "mlp_w1": nrm(ks[8], (DEPTH, D, D_FF), f32) * D ** -0.5,
        "mlp_w2": nrm(ks[9], (DEPTH, D_FF, D), f32) * D_FF ** -0.5,
        "kv_norm_g": 1.0 + 0.02 * nrm(ks[10], (D,), f32),
        "kv_w": nrm(ks[11], (D, 2 * D), f32) * D ** -0.5,
        "kv_k_g": 1.0 + 0.02 * nrm(ks[12], (Dh,), f32),
        "b_norm_g": 1.0 + 0.02 * nrm(ks[13], (N_B, D), f32),
        "b_w_q": nrm(ks[14], (N_B, D, D), f32) * D ** -0.5,
        "b_q_g": 1.0 + 0.02 * nrm(ks[15], (N_B, Dh), f32),
        "b_rel": 0.5 * nrm(ks[16], (N_B, H, N_REL), f32),
        "b_w_out": nrm(ks[17], (N_B, D, D), f32) * D ** -0.5,
    }


def reference(x, a_norm_g, a_w_in, a_b_f, a_q_g, a_k_g, a_w_out,
              mlp_norm_g, mlp_w1, mlp_w2,
              kv_norm_g, kv_w, kv_k_g,
              b_norm_g, b_w_q, b_q_g, b_rel, b_w_out):
    B, S, D = x.shape
    H, Dh = N_HEADS, HEAD_DIM
    h = x
    layer = 0
    for l in range(N_A):
        u = rms_norm(h, a_norm_g[l])
        proj = u @ a_w_in[l]
        q, k, v, fz = jnp.split(proj, [D, 2 * D, 3 * D], axis=-1)
        q = rms_norm(q.reshape(B, S, H, Dh), a_q_g[l])
        k = rms_norm(k.reshape(B, S, H, Dh), a_k_g[l])
        v = v.reshape(B, S, H, Dh)
        logf = jax.nn.log_sigmoid(fz.astype(jnp.float32) + a_b_f[l].astype(jnp.float32))
        o = forgetting_attention(q, k, v, logf)
        h = h + o.reshape(B, S, D) @ a_w_out[l]
        h = h + sq_relu_mlp(h, mlp_norm_g[layer], mlp_w1[layer], mlp_w2[layer])
        layer += 1
    kv = rms_norm(h, kv_norm_g) @ kv_w
    k_sh, v_sh = jnp.split(kv, [D], axis=-1)
    k_sh = rms_norm(k_sh.reshape(B, S, H, Dh), kv_k_g)
    v_sh = v_sh.reshape(B, S, H, Dh)
    for l in range(N_B):
        u = rms_norm(h, b_norm_g[l])
        q = rms_norm((u @ b_w_q[l]).reshape(B, S, H, Dh), b_q_g[l])
        o = chunked_relbias_attention(q, k_sh, v_sh, b_rel[l])
        h = h + o.reshape(B, S, D) @ b_w_out[l]
        h = h + sq_relu_mlp(h, mlp_norm_g[layer], mlp_w1[layer], mlp_w2[layer])
        layer += 1
    return h
```

```python
import numpy as np
from contextlib import ExitStack
import concourse.bass as bass
import concourse.mybir as mybir
from concourse.bass_utils import run_bass_kernel_spmd

F32 = mybir.dt.float32
BF16 = mybir.dt.bfloat16
AF = mybir.ActivationFunctionType
ALU = mybir.AluOpType

D = 2048
H = 16
DH = 128
DFF = 8192
TB = 512
EPS = 1e-6
NEG = -30000.0
ENGS = ("pe", "act", "dve", "pool", "sp")


class Rec:
    __slots__ = ("eng", "fn", "waits", "signal", "dma", "slot", "val", "sigval")


class Buf:
    __slots__ = ("name", "w", "rs", "slot")

    def __init__(self, name):
        self.name = name
        self.w = None
        self.rs = {}
        self.slot = None


class DSlot:
    __slots__ = ("cnt", "handle", "last")

    def __init__(self):
        self.cnt = 0
        self.handle = None
        self.last = None


class KB:
    def __init__(self):
        self.q = {e: [] for e in ENGS}
        self.dslots = []
        self.pslots = []
        self.phase_slot_i = 0
        self.bufs = []

    def buf(self, name):
        b = Buf(name)
        self.bufs.append(b)
        return b

    def pbuf(self, name):
        b = Buf(name)
        b.slot = DSlot()
        self.pslots.append(b.slot)
        return b

    def _slot_for(self, b):
        if b.slot is None:
            if self.phase_slot_i >= len(self.dslots):
                self.dslots.append(DSlot())
            b.slot = self.dslots[self.phase_slot_i]
            self.phase_slot_i += 1
        return b.slot

    def op(self, eng, fn, reads=(), writes=(), dma=False, sembuf=None, nodep=False):
        r = Rec()
        r.eng = eng
        r.fn = fn
        r.signal = False
        r.dma = dma
        r.slot = None
        r.val = 0
        r.sigval = 0
        deps = []
        if not nodep:
            for b in reads:
                if b.w is not None:
                    deps.append(b.w)
            for b in writes:
                if b.w is not None:
                    deps.append(b.w)
                deps.extend(b.rs.values())
        waits = []
        seen = set()
        for d in deps:
            if d is r or id(d) in seen:
                continue
            seen.add(id(d))
            if (not d.dma) and d.eng == "pe" and eng == "pe":
                continue
            if not d.dma:
                d.signal = True
            waits.append(d)
        r.waits = waits
        if dma:
            sb = sembuf
            if sb is None:
                sb = writes[0]
            s = self._slot_for(sb)
            s.cnt += 16
            r.slot = s
            r.val = s.cnt
            s.last = r
        for b in reads:
            b.rs[("d", id(r)) if dma else eng] = r
        for b in writes:
            b.w = r
            b.rs = {}
        self.q[eng].append(r)
        return r

    def barrier(self):
        targets = []
        for e in ("pe", "act", "dve", "pool"):
            for r in reversed(self.q[e]):
                if r.fn is not None and not r.dma:
                    r.signal = True
                    targets.append(r)
                    break
        for s in self.dslots:
            if s.last is not None:
                targets.append(s.last)
        for e in ENGS:
            r = Rec()
            r.eng = e
            r.fn = None
            r.signal = False
            r.dma = False
            r.slot = None
            r.val = 0
            r.sigval = 0
            r.waits = [t for t in targets if not (t.eng == e and not t.dma)]
            self.q[e].append(r)
        for b in self.bufs:
            b.w = None
            b.rs = {}
            b.slot = None
        self.bufs = []
        self.phase_slot_i = 0

    def finalize(self, nc, stack):
        self.esem = {}
        for e in ENGS:
            self.esem[e] = stack.enter_context(nc.semaphore("es_" + e))
            c = 0
            for r in self.q[e]:
                if r.signal and not r.dma and r.fn is not None:
                    c += 1
                r.sigval = c if (r.signal and not r.dma) else 0
        for i, s in enumerate(self.dslots):
            s.handle = stack.enter_context(nc.semaphore("ds_%d" % i))
        for i, s in enumerate(self.pslots):
            s.handle = stack.enter_context(nc.semaphore("pw_%d" % i))

    def replay(self, eng, e):
        seen = {}
        for r in self.q[eng]:
            for d in r.waits:
                if d.dma:
                    key = id(d.slot)
                    h = d.slot.handle
                    val = d.val
                else:
                    key = d.eng
                    h = self.esem[d.eng]
                    val = d.sigval
                if seen.get(key, 0) >= val:
                    continue
                seen[key] = val
                e.wait_ge(h, val)
            if r.fn is not None:
                ins = r.fn(e)
                if r.dma:
                    ins.then_inc(r.slot.handle, 16)
                elif r.signal:
                    ins.then_inc(self.esem[eng], 1)


def MM(out, lhsT, rhs, start, stop, skip=False):
    if skip:
        return lambda e: e.matmul(out, lhsT=lhsT, rhs=rhs, start=start, stop=stop, skip_group_check=True)
    return lambda e: e.matmul(out, lhsT=lhsT, rhs=rhs, start=start, stop=stop)


def TR(out, in_, ident):
    return lambda e: e.transpose(out=out, in_=in_, identity=ident)


def ACT(out, in_, func, **kw):
    return lambda e: e.activation(out=out, in_=in_, func=func, **kw)


def TT(out, in0, in1, op):
    return lambda e: e.tensor_tensor(out=out, in0=in0, in1=in1, op=op)


def TS(out, in0, s1, op0, s2=None, op1=None):
    if op1 is None:
        return lambda e: e.tensor_scalar(out=out, in0=in0, scalar1=s1, scalar2=None, op0=op0)
    return lambda e: e.tensor_scalar(out=out, in0=in0, scalar1=s1, scalar2=s2, op0=op0, op1=op1)


def STT(out, in0, scalar, in1, op0, op1):
    return lambda e: e.scalar_tensor_tensor(out=out, in0=in0, scalar=scalar, in1=in1, op0=op0, op1=op1)


def CP(out, in_):
    return lambda e: e.tensor_copy(out=out, in_=in_)


def ACP(out, in_):
    return lambda e: e.activation(out=out, in_=in_, func=AF.Copy)


def RCP(out, in_):
    return lambda e: e.reciprocal(out=out, in_=in_)


def MSET(ap, c):
    return lambda e: e.memset(ap, c)


def DMA(out, in_, **kw):
    return lambda e: e.dma_start(out=out, in_=in_, **kw)


class Tile:
    __slots__ = ("ap", "buf")

    def __init__(self, ap, buf):
        self.ap = ap
        self.buf = buf


class Arena:
    def __init__(self, kb, t32, nwords):
        self.kb = kb
        self.t32 = t32
        self.t16 = t32.bitcast(BF16)
        self.nbytes = nwords * 4
        self.top = 0
        self.mark = 0

    def tile(self, dtype, shape, name):
        n = 1
        for s in shape:
            n *= s
        esz = 4 if dtype == F32 else 2
        nb = (n * esz + 63) // 64 * 64
        off = self.top
        assert off + nb <= self.nbytes, ("arena overflow", name, off, nb, self.nbytes)
        self.top += nb
        if dtype == F32:
            ap = self.t32[:, off // 4: off // 4 + n]
        else:
            ap = self.t16[:, off // 2: off // 2 + n]
        if len(shape) == 2:
            ap = ap.rearrange("p (a b) -> p a b", b=shape[1])
        elif len(shape) == 3:
            ap = ap.rearrange("p (a b c) -> p a b c", b=shape[1], c=shape[2])
        return Tile(ap, self.kb.buf(name))

    def set_mark(self):
        self.mark = self.top

    def reset(self):
        self.top = self.mark


def build(S, debug=False, nph=99):
    NT = S // 128
    NB = S // TB
    nc = bass.Bass("TRN2", target_bir_lowering=False)
    kb = KB()

    def din(name, shape, dt=F32):
        return nc.dram_tensor(name, shape, dt, kind="ExternalInput").ap()

    def dscr(name, shape, dt):
        return nc.dram_tensor(name, shape, dt, kind=("ExternalOutput" if debug else "Internal")).ap()

    x = din("x", [S, D])
    a_norm_g = din("a_norm_g", [D])
    a_w_in = din("a_w_in", [D, 3 * D + H])
    a_b_f = din("a_b_f", [H])
    a_q_g = din("a_q_g", [DH])
    a_k_g = din("a_k_g", [DH])
    a_w_out = din("a_w_out", [D, D])
    mlp_norm_g = din("mlp_norm_g", [2, D])
    mlp_w1 = din("mlp_w1", [2, D, DFF])
    mlp_w2 = din("mlp_w2", [2, DFF, D])
    kv_norm_g = din("kv_norm_g", [D])
    kv_w = din("kv_w", [D, 2 * D])
    kv_k_g = din("kv_k_g", [DH])
    b_norm_g = din("b_norm_g", [D])
    b_w_q = din("b_w_q", [D, D])
    b_q_g = din("b_q_g", [DH])
    b_w_out = din("b_w_out", [D, D])
    BMx = din("BMx", [H, 128, 8 * 512])
    c_ident = din("c_ident", [128, 128])
    c_tri = din("c_tri", [128, 128])
    c_trimask = din("c_trimask", [128, 128])
    y = nc.dram_tensor("y", [S, D], F32, kind="ExternalOutput").ap()

    def wscr(name, K, N):
        return dscr(name, [(K // 1024) * (N // 512), 128, 4096], BF16)

    w_in_b = wscr("w_in_b", D, 3 * D)
    wf_b = dscr("wf_b", [128, 16 * H], BF16)
    w_outA_b = wscr("w_outA_b", D, D)
    w1_b = [wscr("w1_b%d" % l, D, DFF) for l in range(2)]
    w2_b = [wscr("w2_b%d" % l, DFF, D) for l in range(2)]
    kv_w_b = wscr("kv_w_b", D, 2 * D)
    b_wq_b = wscr("b_wq_b", D, D)
    w_outB_b = wscr("w_outB_b", D, D)

    qT = dscr("qT", [H, 128, S], BF16)
    kT = dscr("kT", [H, 128, S], BF16)
    vv = dscr("vv", [H, 128, NT * 128], BF16)
    cdr = dscr("cdr", [S, H], F32)
    cTd = dscr("cTd", [H, S], F32)
    oT = dscr("oT", [D, S], BF16)
    h1 = dscr("h1", [S, D], F32)
    h2 = dscr("h2", [S, D], F32)
    h3 = dscr("h3", [S, D], F32)

    stack = ExitStack()
    NW = 52400
    arena_t = stack.enter_context(nc.sbuf_tensor("arena", [128, NW], F32))
    ps_t = [stack.enter_context(nc.psum_tensor("ps%d" % i, [128, 512], F32)) for i in range(8)]
    ps16_t = [t.bitcast(BF16) for t in ps_t]
    psb = [Buf("ps%d" % i) for i in range(8)]
    ar = Arena(kb, arena_t, NW)

    wbufs = {}

    pending_casts = []

    def cast_weight(name, src, dst, K, N, fine=False, defer=False):
        ncb = N // 512
        nkg = K // 1024
        if fine:
            bl = [kb.pbuf("w_%s_%d" % (name, i)) for i in range(nkg * ncb)]
        else:
            b = kb.pbuf("w_" + name)
            bl = [b] * (nkg * ncb)
        wbufs[name] = bl
        order = [(kg, cb) for cb in range(ncb) for kg in range(nkg)] if fine else [(kg, cb) for kg in range(nkg) for cb in range(ncb)]
        for kg, cb in order:
            s_ap = src[kg * 1024:(kg + 1) * 1024, cb * 512:(cb + 1) * 512].rearrange("(kc p) n -> p kc n", p=128)
            d_ap = dst[kg * ncb + cb].rearrange("p (kc n) -> p kc n", n=512)
            b = bl[kg * ncb + cb]

            def emit(d_ap=d_ap, s_ap=s_ap, b=b):
                kb.op("pool", DMA(d_ap, s_ap), writes=[b], dma=True, sembuf=b, nodep=True)
            if defer:
                pending_casts.append((name, emit))
            else:
                emit()

    def pump(n):
        for _ in range(n):
            if pending_casts:
                pending_casts.pop(0)[1]()

    def flush_casts(name):
        while any(nm == name for nm, _ in pending_casts):
            pending_casts.pop(0)[1]()

    bwf = kb.pbuf("w_wf")
    wbufs["wf"] = bwf
    kb.op("pool", DMA(wf_b.rearrange("p (kc n) -> p kc n", n=H),
                      a_w_in[:, 3 * D:3 * D + H].rearrange("(kc p) n -> p kc n", p=128)),
          writes=[bwf], dma=True, sembuf=bwf, nodep=True)
    cast_weight("w_in", a_w_in[:, 0:3 * D], w_in_b, D, 3 * D, fine=True)

    cast_weight("w_outA", a_w_out, w_outA_b, D, D, defer=True)
    cast_weight("w1_0", mlp_w1[0], w1_b[0], D, DFF, defer=True)
    cast_weight("w2_0", mlp_w2[0], w2_b[0], DFF, D, defer=True)
    cast_weight("kv_w", kv_w, kv_w_b, D, 2 * D, defer=True)
    cast_weight("b_wq", b_w_q, b_wq_b, D, D, defer=True)
    cast_weight("w_outB", b_w_out, w_outB_b, D, D, defer=True)
    cast_weight("w1_1", mlp_w1[1], w1_b[1], D, DFF, defer=True)
    cast_weight("w2_1", mlp_w2[1], w2_b[1], DFF, D, defer=True)

    identF = ar.tile(F32, [128], "identF")
    identB = ar.tile(BF16, [128], "identB")
    onesF = ar.tile(F32, [128], "onesF")
    onesB = ar.tile(BF16, [128], "onesB")
    kb.op("sp", DMA(identF.ap, c_ident), writes=[identF.buf], dma=True)
    kb.op("dve", CP(identB.ap, identF.ap), reads=[identF.buf], writes=[identB.buf])
    kb.op("dve", MSET(onesF.ap, 1.0), writes=[onesF.buf])
    kb.op("dve", MSET(onesB.ap, 1.0), writes=[onesB.buf])
    ar.set_mark()
    kb.barrier()

    def bcast_row(vec_ap, n):
        return bass.AP(tensor=vec_ap.tensor, offset=vec_ap.offset, ap=[[0, 128], [1, n]])

    def col_ap(vec_ap, n=128):
        return bass.AP(tensor=vec_ap.tensor, offset=vec_ap.offset, ap=[[1, n], [1, 1]])

    NWS = 5

    class WStream:
        def __init__(self):
            self.slots = [ar.tile(BF16, [8, 512], "wslot%d" % i) for i in range(NWS)]
            self.i = 0

        def load(self, wdram, ti, wbuf):
            t = self.slots[self.i % NWS]
            self.i += 1
            kb.op("sp", DMA(t.ap, wdram[ti].rearrange("p (kc n) -> p kc n", n=512)),
                  reads=[wbuf[ti]], writes=[t.buf], dma=True, sembuf=t.buf)
            return t

    def fe_norm(hin, blk, tt, gbcs, uTs, hx_slots, u_slots, junk, small):
        r0 = blk * TB + tt * 128
        hx = hx_slots[tt % len(hx_slots)]
        kb.op("sp", DMA(hx.ap, hin[r0:r0 + 128, :]), writes=[hx.buf], dma=True)
        ss, lnv, rstd = small[tt % 2]
        kb.op("act", ACT(junk.ap, hx.ap, AF.Square, accum_out=ss.ap), reads=[hx.buf], writes=[junk.buf, ss.buf])
        kb.op("act", ACT(lnv.ap, ss.ap, AF.Ln, scale=1.0 / D, bias=EPS), reads=[ss.buf], writes=[lnv.buf])
        kb.op("act", ACT(rstd.ap, lnv.ap, AF.Exp, scale=-0.5), reads=[lnv.buf], writes=[rstd.buf])
        for gi, gbc in enumerate(gbcs):
            u = u_slots[(tt * len(gbcs) + gi) % len(u_slots)]
            kb.op("dve", STT(u.ap, hx.ap, rstd.ap[:, 0:1], gbc.ap, ALU.mult, ALU.mult),
                  reads=[hx.buf, rstd.buf, gbc.buf], writes=[u.buf])

    def fe_transpose(hin, blk, tt, gbcs, uTs, hx_slots, u_slots, junk, small):
        for gi, gbc in enumerate(gbcs):
            u = u_slots[(tt * len(gbcs) + gi) % len(u_slots)]
            uT, uTb = uTs[gi]
            for half in range(2):
                pk = half
                for j in range(8):
                    kc = half * 8 + j
                    kb.op("pe", TR(ps16_t[pk][:, j * 128:(j + 1) * 128], u.ap[:, kc * 128:(kc + 1) * 128], identB.ap),
                          reads=[u.buf], writes=[psb[pk]])
                dst = uT.ap[:, half * 8:(half + 1) * 8, tt * 128:(tt + 1) * 128]
                src = ps16_t[pk][:, 0:1024].rearrange("p (a b) -> p a b", b=128)
                if half == 0:
                    kb.op("act", ACP(dst, src), reads=[psb[pk]], writes=[uTb[tt]])
                else:
                    kb.op("dve", CP(dst, src), reads=[psb[pk]], writes=[uTb[tt]])

    def front_end_tile(*a):
        fe_norm(*a)
        fe_transpose(*a)

    def front_end(hin, blk, gbcs, uTs, hx_slots, u_slots, junk, small):
        for tt in range(4):
            front_end_tile(hin, blk, tt, gbcs, uTs, hx_slots, u_slots, junk, small)

    def make_uT(name):
        t = ar.tile(BF16, [16, 512], name)
        return (t, [kb.buf(name + "_%d" % i) for i in range(4)])

    def phase_proj(hin, gains, head_specs, v_spec, fz):
        ar.reset()
        ws = WStream()
        gbcs = []
        for gi, g in enumerate(gains):
            t = ar.tile(F32, [D], "gbc%d" % gi)
            kb.op("sp", DMA(t.ap, bcast_row(g, D)), writes=[t.buf], dma=True)
            gbcs.append(t)
        gcols = []
        for hi, hs in enumerate(head_specs):
            t = ar.tile(F32, [1], "gcol%d" % hi)
            kb.op("sp", DMA(t.ap, col_ap(hs[4])), writes=[t.buf], dma=True)
            if hs[5] != 1.0:
                t2 = ar.tile(F32, [1], "gcols%d" % hi)
                kb.op("dve", TS(t2.ap, t.ap, float(hs[5]), ALU.mult), reads=[t.buf], writes=[t2.buf])
                t = t2
            gcols.append(t)
        hx_slots = [ar.tile(F32, [D], "hx%d" % i) for i in range(2)]
        u_slots = [ar.tile(BF16, [D], "u%d" % i) for i in range(2 * len(gains))]
        junk = ar.tile(BF16, [D], "junk")
        small = [tuple(ar.tile(F32, [1], "sm%d_%d" % (i, j)) for j in range(3)) for i in range(2)]
        uT_sl = [[make_uT("uT%d_%d" % (gi, s)) for gi in range(len(gains))] for s in range(2)]
        sq_sl = [ar.tile(BF16, [512], "sq%d" % i) for i in range(2)]
        lnv_sl = [ar.tile(F32, [512], "lnv%d" % i) for i in range(2)]
        rs_sl = [ar.tile(F32, [512], "rs%d" % i) for i in range(2)]
        qn_sl = [ar.tile(BF16, [512], "qn%d" % i) for i in range(3)]
        vst_sl = [ar.tile(BF16, [512], "vst%d" % i) for i in range(3)]
        if fz:
            wf_t = ar.tile(BF16, [16, H], "wf")
            kb.op("sp", DMA(wf_t.ap, wf_b.rearrange("p (kc n) -> p kc n", n=H)), reads=[wbufs["wf"]], writes=[wf_t.buf], dma=True)
            bfb = ar.tile(F32, [H], "bfb")
            kb.op("sp", DMA(bfb.ap, bcast_row(a_b_f, H)), writes=[bfb.buf], dma=True)
            triF = ar.tile(F32, [128], "triF")
            kb.op("sp", DMA(triF.ap, c_tri), writes=[triF.buf], dma=True)
            nsp_all = ar.tile(F32, [NT, H], "nsp_all")
            nsp_bufs = [kb.buf("nsp%d" % i) for i in range(NT)]
            z_sl = [ar.tile(F32, [H], "z%d" % i) for i in range(2)]
            e_sl = [ar.tile(F32, [H], "e%d" % i) for i in range(2)]
            c_sl = [ar.tile(F32, [H], "c%d" % i) for i in range(2)]
            cT_sl = [ar.tile(F32, [128], "cT%d" % i) for i in range(2)]
        cnt = {"h": 0, "v": 0, "f": 0}
        HB = [2, 3, 4]
        SB = [5, 6]
        front_end(hin, 0, gbcs, uT_sl[0], hx_slots, u_slots, junk, small)
        for blk in range(NB):
            uTs = uT_sl[blk % 2]
            pend = None
            items = []
            for hi, hs in enumerate(head_specs):
                for cbl in range(4):
                    items.append((hi, cbl))
            for idx, (hi, cbl) in enumerate(items):
                if blk + 1 < NB:
                    fe_args = (hin, blk + 1, idx // 2, gbcs, uT_sl[(blk + 1) % 2], hx_slots, u_slots, junk, small)
                    if idx % 2 == 0:
                        fe_norm(*fe_args)
                    else:
                        fe_transpose(*fe_args)
                wdram, wbuf, cb0, gi, _, _, outd = head_specs[hi]
                ncb = wdram.shape[0] // 2
                uT, uTb = uTs[gi]
                wt = [ws.load(wdram, kg * ncb + cb0 + cbl, wbuf) for kg in range(2)]
                for hh in range(4):
                    head = cbl * 4 + hh
                    i = cnt["h"]
                    cnt["h"] += 1
                    pb = HB[i % 3]
                    for kc in range(16):
                        kb.op("pe", MM(ps_t[pb][:, :], wt[kc // 8].ap[:, kc % 8, hh * 128:(hh + 1) * 128], uT.ap[:, kc, :],
                                       kc == 0, kc == 15),
                              reads=[wt[kc // 8].buf] + uTb, writes=[psb[pb]])
                    sq = sq_sl[i % 2]
                    kb.op("act", ACT(sq.ap, ps_t[pb][:, :], AF.Square), reads=[psb[pb]], writes=[sq.buf])
                    if pend is not None:
                        pend()
                    def fin(i=i, pb=pb, sq=sq, hi=hi, head=head, outd=outd, blk=blk):
                        sb = SB[i % 2]
                        kb.op("pe", MM(ps_t[sb][:, :], onesB.ap, sq.ap, True, True), reads=[sq.buf], writes=[psb[sb]])
                        lnv = lnv_sl[i % 2]
                        rs = rs_sl[i % 2]
                        kb.op("act", ACT(lnv.ap, ps_t[sb][:, :], AF.Ln, scale=1.0 / DH, bias=EPS), reads=[psb[sb]], writes=[lnv.buf])
                        kb.op("act", ACT(rs.ap, lnv.ap, AF.Exp, scale=-0.5), reads=[lnv.buf], writes=[rs.buf])
                        qn = qn_sl[i % 3]
                        kb.op("dve", STT(qn.ap, ps_t[pb][:, :], gcols[hi].ap[:, 0:1], rs.ap, ALU.mult, ALU.mult),
                              reads=[psb[pb], rs.buf, gcols[hi].buf], writes=[qn.buf])
                        kb.op("pool", DMA(outd[head, :, blk * TB:(blk + 1) * TB], qn.ap), reads=[qn.buf], dma=True, sembuf=qn.buf)
                    pend = fin
            if pend is not None:
                pend()
                pend = None
            if v_spec is not None:
                wdram, wbuf, cb0, gi = v_spec
                ncb = wdram.shape[0] // 2
                uT, uTb = uTs[gi]
                for cbl in range(4):
                    wt = [ws.load(wdram, kg * ncb + cb0 + cbl, wbuf) for kg in range(2)]
                    for tt in range(4):
                        i = cnt["v"]
                        cnt["v"] += 1
                        pb = HB[(cnt["h"] + i) % 3]
                        for kc in range(16):
                            kb.op("pe", MM(ps_t[pb][:, :], uT.ap[:, kc, tt * 128:(tt + 1) * 128], wt[kc // 8].ap[:, kc % 8, :],
                                           kc == 0, kc == 15),
                                  reads=[wt[kc // 8].buf, uTb[tt]], writes=[psb[pb]])
                        vst = vst_sl[i % 3]
                        kb.op("dve", CP(vst.ap, ps_t[pb][:, :]), reads=[psb[pb]], writes=[vst.buf])
                        r0 = blk * TB + tt * 128
                        gt_ = blk * 4 + tt
                        v_dst = bass.AP(tensor=vv.tensor, offset=vv.offset + (cbl * 4) * 128 * NT * 128 + gt_ * 128,
                                        ap=[[NT * 128, 128], [128 * NT * 128, 4], [1, 128]])
                        kb.op("pool", DMA(v_dst, vst.ap.rearrange("p (h d) -> p h d", d=128)), reads=[vst.buf], dma=True, sembuf=vst.buf)
            if fz:
                uT, uTb = uTs[0]
                for tt in range(4):
                    gt = blk * 4 + tt
                    i = cnt["f"]
                    cnt["f"] += 1
                    pb = 7
                    for kc in range(16):
                        kb.op("pe", MM(ps_t[pb][:, 0:H], uT.ap[:, kc, tt * 128:(tt + 1) * 128], wf_t.ap[:, kc, :], kc == 0, kc == 15),
                              reads=[wf_t.buf, uTb[tt]], writes=[psb[pb]])
                    z = z_sl[i % 2]
                    e_ = e_sl[i % 2]
                    kb.op("dve", TT(z.ap, ps_t[pb][:, 0:H], bfb.ap, ALU.add), reads=[psb[pb], bfb.buf], writes=[z.buf])
                    kb.op("act", ACT(e_.ap, z.ap, AF.Exp, scale=-1.0), reads=[z.buf], writes=[e_.buf])
                    kb.op("act", ACT(z.ap, e_.ap, AF.Ln, bias=1.0), reads=[e_.buf], writes=[z.buf])
                    kb.op("dve", TS(nsp_all.ap[:, gt, :], z.ap, -1.0, ALU.mult), reads=[z.buf], writes=[nsp_bufs[gt]])
                    for j in range(gt + 1):
                        lhs = triF.ap if j == gt else onesF.ap
                        kb.op("pe", MM(ps_t[pb][:, 32:32 + H], lhs, nsp_all.ap[:, j, :], j == 0, j == gt),
                              reads=[nsp_bufs[j], triF.buf], writes=[psb[pb]])
                    c_ = c_sl[i % 2]
                    kb.op("dve", CP(c_.ap, ps_t[pb][:, 32:32 + H]), reads=[psb[pb]], writes=[c_.buf])
                    kb.op("pool", DMA(cdr[gt * 128:(gt + 1) * 128, :], c_.ap), reads=[c_.buf], dma=True, sembuf=c_.buf)
                    cT_ = cT_sl[i % 2]
                    kb.op("pe", TR(ps_t[pb][0:H, 64:192], c_.ap, identF.ap), reads=[c_.buf], writes=[psb[pb]])
                    kb.op("dve", CP(cT_.ap[0:H, :], ps_t[pb][0:H, 64:192]), reads=[psb[pb]], writes=[cT_.buf])
                    kb.op("pool", DMA(cTd[:, gt * 128:(gt + 1) * 128], cT_.ap[0:H, :]), reads=[cT_.buf], dma=True, sembuf=cT_.buf)
        kb.barrier()

    def phase_attn(fox):
        ar.reset()
        q_sl = [ar.tile(BF16, [S], "qh%d" % i) for i in range(2)]
        k_sl = [ar.tile(BF16, [S], "kh%d" % i) for i in range(2)]
        v_sl = [ar.tile(BF16, [NT, 128], "vh%d" % i) for i in range(2)]
        t_sl = [ar.tile(F32, [512], "t%d" % i) for i in range(6)]
        p_sl = [ar.tile(BF16, [512], "p%d" % i) for i in range(8)]
        ld_sl = [ar.tile(F32, [512], "ld%d" % i) for i in range(2)]
        SK = 5
        rd_sl = [ar.tile(F32, [512], "rd%d" % i) for i in range(2)]
        o_sl = [ar.tile(BF16, [512], "o%d" % i) for i in range(4)]
        if fox:
            c_all = ar.tile(F32, [NT, H], "c_all")
            nc_all = ar.tile(F32, [NT, H], "nc_all")
            kb.op("sp", DMA(c_all.ap, cdr.rearrange("(t p) h -> p t h", p=128)), writes=[c_all.buf], dma=True)
            kb.op("dve", TS(nc_all.ap, c_all.ap, -1.0, ALU.mult), reads=[c_all.buf], writes=[nc_all.buf])
            trim = ar.tile(F32, [128], "trim")
            kb.op("sp", DMA(trim.ap, c_trimask), writes=[trim.buf], dma=True)
            trimB = ar.tile(BF16, [128], "trimB")
            kb.op("dve", CP(trimB.ap, trim.ap), reads=[trim.buf], writes=[trimB.buf])
            cq_sl = [ar.tile(F32, [S], "cq%d" % i) for i in range(2)]
        else:
            bm_sl = [ar.tile(F32, [8, 512], "bm%d" % i) for i in range(2)]
        STB = [0, 1, 2]
        OB = [3, 4]
        DB = [5, 6]

        prep_q = []

        def head_prep(h):
            s = h % 2
            kb.op("sp", DMA(q_sl[s].ap, qT[h]), writes=[q_sl[s].buf], dma=True)
            kb.op("sp", DMA(k_sl[s].ap, kT[h]), writes=[k_sl[s].buf], dma=True)
            kb.op("sp", DMA(v_sl[s].ap, vv[h].rearrange("p (t d) -> p t d", d=128)),
                  writes=[v_sl[s].buf], dma=True)
            if fox:
                cq = cq_sl[s]
                kb.op("sp", DMA(cq.ap, bass.AP(tensor=cTd.tensor, offset=cTd.offset + h * S, ap=[[0, 128], [1, S]])),
                      writes=[cq.buf], dma=True)
            else:
                kb.op("sp", DMA(bm_sl[s].ap, BMx[h].rearrange("p (m q) -> p m q", q=512)), writes=[bm_sl[s].buf], dma=True)

        tiles = []
        for h in range(H):
            for qb in range(NB):
                if fox:
                    kts = list(range(0, 4 * qb + 4))
                else:
                    kts = list(range(max(0, 4 * qb - 4), 4 * qb + 4))
                for n, kt in enumerate(kts):
                    tiles.append((h, qb, kt, n == 0, n == len(kts) - 1))
        NTL = len(tiles)
        qcount = [0]

        B2R = [(0, 128), (0, 256), (0, 384), (0, 512), (0, 512), (128, 512), (256, 512), (384, 512)]

        def cols(qb, kt):
            if fox:
                return ((kt - 4 * qb) * 128 if kt >= 4 * qb else 0), 512
            return B2R[kt - (4 * qb - 4)]

        def stageABC(i):
            h, qb, kt, first, last = tiles[i]
            s = h % 2
            c0, c1 = cols(qb, kt)
            sb = STB[i % 3]
            q0 = qb * TB
            diag = fox and kt >= 4 * qb
            kb.op("pe", MM(ps_t[sb][:, c0:c1], k_sl[s].ap[:, kt * 128:(kt + 1) * 128], q_sl[s].ap[:, q0 + c0:q0 + c1], True, not diag),
                  reads=[k_sl[s].buf, q_sl[s].buf], writes=[psb[sb]])
            if diag:
                kb.op("pe", MM(ps_t[sb][:, c0:c0 + 128], identB.ap, trimB.ap, False, True), reads=[trimB.buf], writes=[psb[sb]])
            t = t_sl[i % 6]
            p = p_sl[i % 8]
            if fox:
                kb.op("dve", STT(t.ap[:, c0:512], ps_t[sb][:, c0:512], nc_all.ap[:, kt, h:h + 1], cq_sl[s].ap[:, q0 + c0:q0 + 512], ALU.add, ALU.add),
                      reads=[psb[sb], cq_sl[s].buf, nc_all.buf], writes=[t.buf])
                kb.op("act", ACT(p.ap[:, c0:512], t.ap[:, c0:512], AF.Exp), reads=[t.buf], writes=[p.buf])
            else:
                mp = 7 - (kt - (4 * qb - 4))
                kb.op("dve", TT(t.ap[:, c0:c1], ps_t[sb][:, c0:c1], bm_sl[s].ap[:, mp, c0:c1], ALU.add), reads=[psb[sb], bm_sl[s].buf], writes=[t.buf])
                kb.op("act", ACT(p.ap[:, c0:c1], t.ap[:, c0:c1], AF.Exp), reads=[t.buf], writes=[p.buf])

        def stageD(i):
            h, qb, kt, first, last = tiles[i]
            s = h % 2
            c0, c1 = cols(qb, kt)
            p = p_sl[i % 8]
            qi = qcount[0]
            ob = OB[qi % 2]
            db = DB[qi % 2]
            kb.op("pe", MM(ps_t[ob][:, c0:c1], v_sl[s].ap[:, kt, :], p.ap[:, c0:c1], first, last, skip=True),
                  reads=[v_sl[s].buf, p.buf], writes=[psb[ob]])
            kb.op("pe", MM(ps_t[db][:, c0:c1], onesB.ap, p.ap[:, c0:c1], first, last, skip=True),
                  reads=[p.buf], writes=[psb[db]])
            if last:
                rd = rd_sl[qi % 2]
                o = o_sl[qi % 4]
                ld = ld_sl[qi % 2]

                def ep_act(ld=ld, rd=rd, db=db):
                    kb.op("act", ACT(ld.ap, ps_t[db][:, :], AF.Ln), reads=[psb[db]], writes=[ld.buf])
                    kb.op("act", ACT(rd.ap, ld.ap, AF.Exp, scale=-1.0), reads=[ld.buf], writes=[rd.buf])

                def ep_dve(o=o, rd=rd, ob=ob, h=h, qb=qb):
                    kb.op("dve", TT(o.ap, ps_t[ob][:, :], rd.ap, ALU.mult), reads=[psb[ob], rd.buf], writes=[o.buf])
                    kb.op("sp", DMA(oT[h * 128:(h + 1) * 128, qb * TB:(qb + 1) * TB], o.ap), reads=[o.buf], dma=True, sembuf=o.buf)
                deferred.append((cur_iter[0] + 1, ep_act))
                deferred.append((cur_iter[0] + 3, ep_dve))
                qcount[0] += 1
                if fox:
                    if qb >= 1:
                        pump(1)
                elif qcount[0] % 3 == 0:
                    pump(1)

        head_first = {}
        for i, tl in enumerate(tiles):
            head_first.setdefault(tl[0], i)
        head_prep(0)
        while prep_q:
            prep_q.pop(0)()
        deferred = []
        cur_iter = [0]

        def run_deferred(upto):
            while deferred and deferred[0][0] <= upto:
                deferred.pop(0)[1]()

        for i in range(NTL + SK):
            cur_iter[0] = i
            if i < NTL:
                hh_ = tiles[i][0]
                if i == head_first[hh_]:
                    while prep_q:
                        prep_q.pop(0)()
                if i == head_first[hh_] + SK and hh_ + 1 < H:
                    head_prep(hh_ + 1)
                if prep_q:
                    prep_q.pop(0)()
                stageABC(i)
            run_deferred(i)
            if i >= SK:
                stageD(i - SK)
        run_deferred(10 ** 9)
        kb.barrier()

    def phase_outproj(wdram, wbuf, hin, hout):
        ar.reset()
        wres = []
        for kg in range(2):
            for cb in range(4):
                t = ar.tile(BF16, [8, 512], "wo%d_%d" % (kg, cb))
                kb.op("sp", DMA(t.ap, wdram[kg * 4 + cb].rearrange("p (kc n) -> p kc n", n=512)), reads=[wbuf[kg * 4 + cb]], writes=[t.buf], dma=True)
                wres.append(t)
        o_sl = [ar.tile(BF16, [16, 512], "ot%d" % i) for i in range(2)]
        hx_sl = [ar.tile(F32, [D], "hx%d" % i) for i in range(3)]
        PB = [0, 1, 2, 3]
        n = 0
        for tt in range(NT):
            otb = o_sl[(tt // 4) % 2]
            hx = hx_sl[tt % 3]
            if tt % 4 == 0:
                kb.op("sp", DMA(otb.ap, oT[:, tt * 128:(tt + 4) * 128].rearrange("(h p) t -> p h t", p=128)), writes=[otb.buf], dma=True)
            kb.op("sp", DMA(hx.ap, hin[tt * 128:(tt + 1) * 128, :]), writes=[hx.buf], dma=True)

            class _V:
                pass
            ot = _V()
            ot.ap = otb.ap[:, :, (tt % 4) * 128:(tt % 4 + 1) * 128]
            ot.buf = otb.buf
            for cb in range(4):
                pb = PB[n % 4]
                n += 1
                for hh in range(16):
                    w = wres[(hh // 8) * 4 + cb]
                    kb.op("pe", MM(ps_t[pb][:, :], ot.ap[:, hh, :], w.ap[:, hh % 8, :], hh == 0, hh == 15),
                          reads=[ot.buf, w.buf], writes=[psb[pb]])
                kb.op("dve", TT(hx.ap[:, cb * 512:(cb + 1) * 512], hx.ap[:, cb * 512:(cb + 1) * 512], ps_t[pb][:, :], ALU.add),
                      reads=[psb[pb], hx.buf], writes=[hx.buf])
            kb.op("pool", DMA(hout[tt * 128:(tt + 1) * 128, :], hx.ap), reads=[hx.buf], dma=True, sembuf=hx.buf)
        kb.barrier()

    def phase_mlp(l, hin, hout):
        ar.reset()
        ws = WStream()
        gbc = ar.tile(F32, [D], "gbc")
        kb.op("sp", DMA(gbc.ap, bcast_row(mlp_norm_g[l], D)), writes=[gbc.buf], dma=True)
        hxk = [ar.tile(F32, [D], "hxk%d" % i) for i in range(4)]
        fe_hx = [ar.tile(F32, [D], "fehx%d" % i) for i in range(2)]
        u_slots = [ar.tile(BF16, [D], "u%d" % i) for i in range(2)]
        junk = ar.tile(BF16, [D], "junk")
        small = [tuple(ar.tile(F32, [1], "sm%d_%d" % (i, j)) for j in range(3)) for i in range(2)]
        uTs = [make_uT("uT")]
        aT = ar.tile(BF16, [64, 512], "aT")
        aTb = [kb.buf("aT%d" % i) for i in range(64)]
        r_sl = [ar.tile(F32, [512], "r%d" % i) for i in range(3)]
        w1d, w2d = w1_b[l], w2_b[l]
        wb1, wb2 = wbufs["w1_%d" % l], wbufs["w2_%d" % l]
        P1 = [2, 3]
        P6 = [2, 3, 4, 5, 6, 7]
        n1 = 0
        front_end(hin, 0, [gbc], uTs, fe_hx, u_slots, junk, small)
        for blk in range(NB):
            uT, uTb = uTs[0]
            for cb in range(16):
                wt = [ws.load(w1d, kg * 16 + cb, wb1) for kg in range(2)]
                for ff in range(4):
                    fc = cb * 4 + ff
                    pb = P1[n1 % 2]
                    r = r_sl[n1 % 3]
                    n1 += 1
                    for kc in range(16):
                        kb.op("pe", MM(ps_t[pb][:, :], wt[kc // 8].ap[:, kc % 8, ff * 128:(ff + 1) * 128], uT.ap[:, kc, :], kc == 0, kc == 15),
                              reads=[wt[kc // 8].buf] + uTb, writes=[psb[pb]])
                    kb.op("act", ACT(r.ap, ps_t[pb][:, :], AF.Relu), reads=[psb[pb]], writes=[r.buf])
                    kb.op("dve", TT(aT.ap[:, fc, :], r.ap, r.ap, ALU.mult), reads=[r.buf], writes=[aTb[fc]])
            for tt in range(4):
                r0 = blk * TB + tt * 128
                kb.op("sp", DMA(hxk[tt].ap, hin[r0:r0 + 128, :]), writes=[hxk[tt].buf], dma=True)
            for cb in range(4):
                fe_args = (hin, blk + 1, cb, [gbc], uTs, fe_hx, u_slots, junk, small)
                if l == 0:
                    pump(1)
                if blk + 1 < NB:
                    fe_norm(*fe_args)
                P2 = [P6[(cb * 4 + tt) % 6] for tt in range(4)]
                for kg in range(8):
                    wt = ws.load(w2d, kg * 4 + cb, wb2)
                    if kg == 4 and blk + 1 < NB:
                        fe_transpose(*fe_args)
                    for fl in range(8):
                        fc = kg * 8 + fl
                        for tt in range(4):
                            kb.op("pe", MM(ps_t[P2[tt]][:, :], aT.ap[:, fc, tt * 128:(tt + 1) * 128], wt.ap[:, fl, :], fc == 0, fc == 63),
                                  reads=[aTb[fc], wt.buf], writes=[psb[P2[tt]]])
                for tt in range(4):
                    hx = hxk[tt]
                    kb.op("dve", TT(hx.ap[:, cb * 512:(cb + 1) * 512], hx.ap[:, cb * 512:(cb + 1) * 512], ps_t[P2[tt]][:, :], ALU.add),
                          reads=[psb[P2[tt]], hx.buf], writes=[hx.buf])
            for tt in range(4):
                r0 = blk * TB + tt * 128
                kb.op("pool", DMA(hout[r0:r0 + 128, :], hxk[tt].ap), reads=[hxk[tt].buf], dma=True, sembuf=hxk[tt].buf)
        kb.barrier()

    SC = float(DH) ** -0.5
    wb = wbufs
    def P1():
        phase_proj(x, [a_norm_g],
                   [(w_in_b, wb["w_in"], 0, 0, a_q_g, SC, qT), (w_in_b, wb["w_in"], 4, 0, a_k_g, 1.0, kT)],
                   (w_in_b, wb["w_in"], 8, 0), True)

    def P3():
        flush_casts("w_outA")
        phase_outproj(w_outA_b, wb["w_outA"], x, h1)

    def P4():
        flush_casts("w2_0")
        phase_mlp(0, h1, h2)

    def P5():
        flush_casts("b_wq")
        phase_proj(h2, [kv_norm_g, b_norm_g],
                   [(kv_w_b, wb["kv_w"], 0, 0, kv_k_g, 1.0, kT), (b_wq_b, wb["b_wq"], 0, 1, b_q_g, SC, qT)],
                   (kv_w_b, wb["kv_w"], 4, 0), False)

    def P7():
        flush_casts("w_outB")
        phase_outproj(w_outB_b, wb["w_outB"], h2, h3)

    def P8():
        flush_casts("w2_1")
        phase_mlp(1, h3, y)

    phases = [P1, lambda: phase_attn(True), P3, P4, P5, lambda: phase_attn(False), P7, P8]
    for ph in phases[:nph]:
        ph()

    kb.finalize(nc, stack)
    with nc.Block() as block:
        block.tensor(lambda e: kb.replay("pe", e))
        block.scalar(lambda e: kb.replay("act", e))
        block.vector(lambda e: kb.replay("dve", e))
        block.gpsimd(lambda e: kb.replay("pool", e))
        block.sync(lambda e: kb.replay("sp", e))
    stack.close()
    return nc


def make_consts(b_rel0):
    ident = np.eye(128, dtype=np.float32)
    tp = np.arange(128)
    tri = (tp[:, None] <= tp[None, :]).astype(np.float32)
    trimask = np.where(tp[None, :] >= tp[:, None], 0.0, NEG).astype(np.float32)
    kl = np.arange(128)[:, None, None]
    mp = np.arange(8)[None, :, None]
    ql = np.arange(512)[None, None, :]
    m = 7 - mp
    dc = 8 - 2 * m + ql // 64 - kl // 64
    valid = (dc >= 0) & (dc <= 8)
    dist = 512 - 128 * m + ql - kl
    idx = np.clip(dist, -63, 256) + 63
    idx = np.where(valid, idx, b_rel0.shape[1])
    table = np.concatenate([b_rel0, np.full((b_rel0.shape[0], 1), NEG, np.float32)], axis=1)
    BMx = np.ascontiguousarray(table[:, idx]).astype(np.float32).reshape(b_rel0.shape[0], 128, 8 * 512)
    return {"c_ident": ident, "c_tri": tri, "c_trimask": trimask, "BMx": BMx}


def make_in_map(xb, inputs, consts):
    m = {
        "x": np.ascontiguousarray(xb),
        "a_norm_g": inputs["a_norm_g"][0], "a_w_in": inputs["a_w_in"][0], "a_b_f": inputs["a_b_f"][0],
        "a_q_g": inputs["a_q_g"][0], "a_k_g": inputs["a_k_g"][0], "a_w_out": inputs["a_w_out"][0],
        "mlp_norm_g": inputs["mlp_norm_g"], "mlp_w1": inputs["mlp_w1"], "mlp_w2": inputs["mlp_w2"],
        "kv_norm_g": inputs["kv_norm_g"], "kv_w": inputs["kv_w"], "kv_k_g": inputs["kv_k_g"],
        "b_norm_g": inputs["b_norm_g"][0], "b_w_q": inputs["b_w_q"][0], "b_q_g": inputs["b_q_g"][0],
        "b_w_out": inputs["b_w_out"][0],
    }
    m.update(consts)
    return {k: np.ascontiguousarray(np.asarray(v, dtype=np.float32)) for k, v in m.items()}


_NC_CACHE = {}


def kernel(**inputs):
    inputs = {k: np.asarray(v) for k, v in inputs.items()}
    x = inputs["x"]
    B, S, _ = x.shape
    if S not in _NC_CACHE:
        _NC_CACHE[S] = build(S)
    nc = _NC_CACHE[S]
    consts = make_consts(np.asarray(inputs["b_rel"][0], dtype=np.float32))
    in_maps = [make_in_map(x[b], inputs, consts) for b in range(B)]
    res = run_bass_kernel_spmd(nc, in_maps, core_ids=list(range(B)))
    return np.stack([r["y"] for r in res.results], axis=0).astype(np.float32)
```

```python
import numpy as np
from contextlib import ExitStack
import concourse.bass as bass
import concourse.mybir as mybir
from concourse.bass_utils import run_bass_kernel_spmd

F32 = mybir.dt.float32
BF16 = mybir.dt.bfloat16
AF = mybir.ActivationFunctionType
ALU = mybir.AluOpType

D = 2048
H = 16
DH = 128
DFF = 8192
TB = 512
EPS = 1e-6
NEG = -30000.0
ENGS = ("pe", "act", "dve", "pool", "sp")


class Rec:
    __slots__ = ("eng", "fn", "waits", "signal", "dma", "slot", "val", "sigval")


class Buf:
    __slots__ = ("name", "w", "rs", "slot")

    def __init__(self, name):
        self.name = name
        self.w = None
        self.rs = {}
        self.slot = None


class DSlot:
    __slots__ = ("cnt", "handle", "last")

    def __init__(self):
        self.cnt = 0
        self.handle = None
        self.last = None


class KB:
    def __init__(self):
        self.q = {e: [] for e in ENGS}
        self.dslots = []
        self.pslots = []
        self.phase_slot_i = 0
        self.bufs = []

    def buf(self, name):
        b = Buf(name)
        self.bufs.append(b)
        return b

    def pbuf(self, name):
        b = Buf(name)
        b.slot = DSlot()
        self.pslots.append(b.slot)
        return b

    def _slot_for(self, b):
        if b.slot is None:
            if self.phase_slot_i >= len(self.dslots):
                self.dslots.append(DSlot())
            b.slot = self.dslots[self.phase_slot_i]
            self.phase_slot_i += 1
        return b.slot

    def op(self, eng, fn, reads=(), writes=(), dma=False, sembuf=None, nodep=False):
        r = Rec()
        r.eng = eng
        r.fn = fn
        r.signal = False
        r.dma = dma
        r.slot = None
        r.val = 0
        r.sigval = 0
        deps = []
        if not nodep:
            for b in reads:
                if b.w is not None:
                    deps.append(b.w)
            for b in writes:
                if b.w is not None:
                    deps.append(b.w)
                deps.extend(b.rs.values())
        waits = []
        seen = set()
        for d in deps:
            if d is r or id(d) in seen:
                continue
            seen.add(id(d))
            if (not d.dma) and d.eng == "pe" and eng == "pe":
                continue
            if not d.dma:
                d.signal = True
            waits.append(d)
        r.waits = waits
        if dma:
            sb = sembuf
            if sb is None:
                sb = writes[0]
            s = self._slot_for(sb)
            s.cnt += 16
            r.slot = s
            r.val = s.cnt
            s.last = r
        for b in reads:
            b.rs[("d", id(r)) if dma else eng] = r
        for b in writes:
            b.w = r
            b.rs = {}
        self.q[eng].append(r)
        return r

    def barrier(self):
        targets = []
        for e in ("pe", "act", "dve", "pool"):
            for r in reversed(self.q[e]):
                if r.fn is not None and not r.dma:
                    r.signal = True
                    targets.append(r)
                    break
        for s in self.dslots:
            if s.last is not None:
                targets.append(s.last)
        for e in ENGS:
            r = Rec()
            r.eng = e
            r.fn = None
            r.signal = False
            r.dma = False
            r.slot = None
            r.val = 0
            r.sigval = 0
            r.waits = [t for t in targets if not (t.eng == e and not t.dma)]
            self.q[e].append(r)
        for b in self.bufs:
            b.w = None
            b.rs = {}
            b.slot = None
        self.bufs = []
        self.phase_slot_i = 0

    def finalize(self, nc, stack):
        self.esem = {}
        for e in ENGS:
            self.esem[e] = stack.enter_context(nc.semaphore("es_" + e))
            c = 0
            for r in self.q[e]:
                if r.signal and not r.dma and r.fn is not None:
                    c += 1
                r.sigval = c if (r.signal and not r.dma) else 0
        for i, s in enumerate(self.dslots):
            s.handle = stack.enter_context(nc.semaphore("ds_%d" % i))
        for i, s in enumerate(self.pslots):
            s.handle = stack.enter_context(nc.semaphore("pw_%d" % i))

    def replay(self, eng, e):
        seen = {}
        for r in self.q[eng]:
            for d in r.waits:
                if d.dma:
                    key = id(d.slot)
                    h = d.slot.handle
                    val = d.val
                else:
                    key = d.eng
                    h = self.esem[d.eng]
                    val = d.sigval
                if seen.get(key, 0) >= val:
                    continue
                seen[key] = val
                e.wait_ge(h, val)
            if r.fn is not None:
                ins = r.fn(e)
                if r.dma:
                    ins.then_inc(r.slot.handle, 16)
                elif r.signal:
                    ins.then_inc(self.esem[eng], 1)


def MM(out, lhsT, rhs, start, stop, skip=False):
    if skip:
        return lambda e: e.matmul(out, lhsT=lhsT, rhs=rhs, start=start, stop=stop, skip_group_check=True)
    return lambda e: e.matmul(out, lhsT=lhsT, rhs=rhs, start=start, stop=stop)


def TR(out, in_, ident):
    return lambda e: e.transpose(out=out, in_=in_, identity=ident)


def ACT(out, in_, func, **kw):
    return lambda e: e.activation(out=out, in_=in_, func=func, **kw)


def TT(out, in0, in1, op):
    return lambda e: e.tensor_tensor(out=out, in0=in0, in1=in1, op=op)


def TS(out, in0, s1, op0, s2=None, op1=None):
    if op1 is None:
        return lambda e: e.tensor_scalar(out=out, in0=in0, scalar1=s1, scalar2=None, op0=op0)
    return lambda e: e.tensor_scalar(out=out, in0=in0, scalar1=s1, scalar2=s2, op0=op0, op1=op1)


def STT(out, in0, scalar, in1, op0, op1):
    return lambda e: e.scalar_tensor_tensor(out=out, in0=in0, scalar=scalar, in1=in1, op0=op0, op1=op1)


def CP(out, in_):
    return lambda e: e.tensor_copy(out=out, in_=in_)


def ACP(out, in_):
    return lambda e: e.activation(out=out, in_=in_, func=AF.Copy)


def RCP(out, in_):
    return lambda e: e.reciprocal(out=out, in_=in_)


def MSET(ap, c):
    return lambda e: e.memset(ap, c)


def DMA(out, in_, **kw):
    return lambda e: e.dma_start(out=out, in_=in_, **kw)


class Tile:
    __slots__ = ("ap", "buf")

    def __init__(self, ap, buf):
        self.ap = ap
        self.buf = buf


class Arena:
    def __init__(self, kb, t32, nwords):
        self.kb = kb
        self.t32 = t32
        self.t16 = t32.bitcast(BF16)
        self.nbytes = nwords * 4
        self.top = 0
        self.mark = 0

    def tile(self, dtype, shape, name):
        n = 1
        for s in shape:
            n *= s
        esz = 4 if dtype == F32 else 2
        nb = (n * esz + 63) // 64 * 64
        off = self.top
        assert off + nb <= self.nbytes, ("arena overflow", name, off, nb, self.nbytes)
        self.top += nb
        if dtype == F32:
            ap = self.t32[:, off // 4: off // 4 + n]
        else:
            ap = self.t16[:, off // 2: off // 2 + n]
        if len(shape) == 2:
            ap = ap.rearrange("p (a b) -> p a b", b=shape[1])
        elif len(shape) == 3:
            ap = ap.rearrange("p (a b c) -> p a b c", b=shape[1], c=shape[2])
        return Tile(ap, self.kb.buf(name))

    def set_mark(self):
        self.mark = self.top

    def reset(self):
        self.top = self.mark


def build(S, debug=False, nph=99):
    NT = S // 128
    NB = S // TB
    nc = bass.Bass("TRN2", target_bir_lowering=False)
    kb = KB()

    def din(name, shape, dt=F32):
        return nc.dram_tensor(name, shape, dt, kind="ExternalInput").ap()

    def dscr(name, shape, dt):
        return nc.dram_tensor(name, shape, dt, kind=("ExternalOutput" if debug else "Internal")).ap()

    x = din("x", [S, D])
    a_norm_g = din("a_norm_g", [D])
    a_w_in = din("a_w_in", [D, 3 * D + H])
    a_b_f = din("a_b_f", [H])
    a_q_g = din("a_q_g", [DH])
    a_k_g = din("a_k_g", [DH])
    a_w_out = din("a_w_out", [D, D])
    mlp_norm_g = din("mlp_norm_g", [2, D])
    mlp_w1 = din("mlp_w1", [2, D, DFF])
    mlp_w2 = din("mlp_w2", [2, DFF, D])
    kv_norm_g = din("kv_norm_g", [D])
    kv_w = din("kv_w", [D, 2 * D])
    kv_k_g = din("kv_k_g", [DH])
    b_norm_g = din("b_norm_g", [D])
    b_w_q = din("b_w_q", [D, D])
    b_q_g = din("b_q_g", [DH])
    b_w_out = din("b_w_out", [D, D])
    BMx = din("BMx", [H, 128, 8 * 512])
    c_ident = din("c_ident", [128, 128])
    c_tri = din("c_tri", [128, 128])
    c_trimask = din("c_trimask", [128, 128])
    y = nc.dram_tensor("y", [S, D], F32, kind="ExternalOutput").ap()

    def wscr(name, K, N):
        return dscr(name, [(K // 1024) * (N // 512), 128, 4096], BF16)

    w_in_b = wscr("w_in_b", D, 3 * D)
    wf_b = dscr("wf_b", [128, 16 * H], BF16)
    w_outA_b = wscr("w_outA_b", D, D)
    w1_b = [wscr("w1_b%d" % l, D, DFF) for l in range(2)]
    w2_b = [wscr("w2_b%d" % l, DFF, D) for l in range(2)]
    kv_w_b = wscr("kv_w_b", D, 2 * D)
    b_wq_b = wscr("b_wq_b", D, D)
    w_outB_b = wscr("w_outB_b", D, D)

    qT = dscr("qT", [H, 128, S], BF16)
    kT = dscr("kT", [H, 128, S], BF16)
    vv = dscr("vv", [H, 128, NT * 128], BF16)
    cdr = dscr("cdr", [S, H], F32)
    cTd = dscr("cTd", [H, S], F32)
    oT = dscr("oT", [D, S], BF16)
    h1 = dscr("h1", [S, D], F32)
    h2 = dscr("h2", [S, D], F32)
    h3 = dscr("h3", [S, D], F32)

    stack = ExitStack()
    NW = 52400
    arena_t = stack.enter_context(nc.sbuf_tensor("arena", [128, NW], F32))
    ps_t = [stack.enter_context(nc.psum_tensor("ps%d" % i, [128, 512], F32)) for i in range(8)]
    ps16_t = [t.bitcast(BF16) for t in ps_t]
    psb = [Buf("ps%d" % i) for i in range(8)]
    ar = Arena(kb, arena_t, NW)

    wbufs = {}

    pending_casts = []

    def cast_weight(name, src, dst, K, N, fine=False, defer=False):
        ncb = N // 512
        nkg = K // 1024
        if fine:
            bl = [kb.pbuf("w_%s_%d" % (name, i)) for i in range(nkg * ncb)]
        else:
            b = kb.pbuf("w_" + name)
            bl = [b] * (nkg * ncb)
        wbufs[name] = bl
        order = [(kg, cb) for cb in range(ncb) for kg in range(nkg)] if fine else [(kg, cb) for kg in range(nkg) for cb in range(ncb)]
        for kg, cb in order:
            s_ap = src[kg * 1024:(kg + 1) * 1024, cb * 512:(cb + 1) * 512].rearrange("(kc p) n -> p kc n", p=128)
            d_ap = dst[kg * ncb + cb].rearrange("p (kc n) -> p kc n", n=512)
            b = bl[kg * ncb + cb]

            def emit(d_ap=d_ap, s_ap=s_ap, b=b):
                kb.op("pool", DMA(d_ap, s_ap), writes=[b], dma=True, sembuf=b, nodep=True)
            if defer:
                pending_casts.append((name, emit))
            else:
                emit()

    def pump(n):
        for _ in range(n):
            if pending_casts:
                pending_casts.pop(0)[1]()

    def flush_casts(name):
        while any(nm == name for nm, _ in pending_casts):
            pending_casts.pop(0)[1]()

    bwf = kb.pbuf("w_wf")
    wbufs["wf"] = bwf
    kb.op("pool", DMA(wf_b.rearrange("p (kc n) -> p kc n", n=H),
                      a_w_in[:, 3 * D:3 * D + H].rearrange("(kc p) n -> p kc n", p=128)),
          writes=[bwf], dma=True, sembuf=bwf, nodep=True)
    cast_weight("w_in", a_w_in[:, 0:3 * D], w_in_b, D, 3 * D, fine=True)

    cast_weight("w_outA", a_w_out, w_outA_b, D, D, defer=True)
    cast_weight("w1_0", mlp_w1[0], w1_b[0], D, DFF, defer=True)
    cast_weight("w2_0", mlp_w2[0], w2_b[0], DFF, D, defer=True)
    cast_weight("kv_w", kv_w, kv_w_b, D, 2 * D, defer=True)
    cast_weight("b_wq", b_w_q, b_wq_b, D, D, defer=True)
    cast_weight("w_outB", b_w_out, w_outB_b, D, D, defer=True)
    cast_weight("w1_1", mlp_w1[1], w1_b[1], D, DFF, defer=True)
    cast_weight("w2_1", mlp_w2[1], w2_b[1], DFF, D, defer=True)

    identF = ar.tile(F32, [128], "identF")
    identB = ar.tile(BF16, [128], "identB")
    onesF = ar.tile(F32, [128], "onesF")
    onesB = ar.tile(BF16, [128], "onesB")
    kb.op("sp", DMA(identF.ap, c_ident), writes=[identF.buf], dma=True)
    kb.op("dve", CP(identB.ap, identF.ap), reads=[identF.buf], writes=[identB.buf])
    kb.op("dve", MSET(onesF.ap, 1.0), writes=[onesF.buf])
    kb.op("dve", MSET(onesB.ap, 1.0), writes=[onesB.buf])
    ar.set_mark()
    kb.barrier()

    def bcast_row(vec_ap, n):
        return bass.AP(tensor=vec_ap.tensor, offset=vec_ap.offset, ap=[[0, 128], [1, n]])

    def col_ap(vec_ap, n=128):
        return bass.AP(tensor=vec_ap.tensor, offset=vec_ap.offset, ap=[[1, n], [1, 1]])

    NWS = 5

    class WStream:
        def __init__(self):
            self.slots = [ar.tile(BF16, [8, 512], "wslot%d" % i) for i in range(NWS)]
            self.i = 0

        def load(self, wdram, ti, wbuf):
            t = self.slots[self.i % NWS]
            self.i += 1
            kb.op("sp", DMA(t.ap, wdram[ti].rearrange("p (kc n) -> p kc n", n=512)),
                  reads=[wbuf[ti]], writes=[t.buf], dma=True, sembuf=t.buf)
            return t

    def fe_norm(hin, blk, tt, gbcs, uTs, hx_slots, u_slots, junk, small):
        r0 = blk * TB + tt * 128
        hx = hx_slots[tt % len(hx_slots)]
        kb.op("sp", DMA(hx.ap, hin[r0:r0 + 128, :]), writes=[hx.buf], dma=True)
        ss, lnv, rstd = small[tt % 2]
        kb.op("act", ACT(junk.ap, hx.ap, AF.Square, accum_out=ss.ap), reads=[hx.buf], writes=[junk.buf, ss.buf])
        kb.op("act", ACT(lnv.ap, ss.ap, AF.Ln, scale=1.0 / D, bias=EPS), reads=[ss.buf], writes=[lnv.buf])
        kb.op("act", ACT(rstd.ap, lnv.ap, AF.Exp, scale=-0.5), reads=[lnv.buf], writes=[rstd.buf])
        for gi, gbc in enumerate(gbcs):
            u = u_slots[(tt * len(gbcs) + gi) % len(u_slots)]
            kb.op("dve", STT(u.ap, hx.ap, rstd.ap[:, 0:1], gbc.ap, ALU.mult, ALU.mult),
                  reads=[hx.buf, rstd.buf, gbc.buf], writes=[u.buf])

    def fe_transpose(hin, blk, tt, gbcs, uTs, hx_slots, u_slots, junk, small):
        for gi, gbc in enumerate(gbcs):
            u = u_slots[(tt * len(gbcs) + gi) % len(u_slots)]
            uT, uTb = uTs[gi]
            for half in range(2):
                pk = half
                for j in range(8):
                    kc = half * 8 + j
                    kb.op("pe", TR(ps16_t[pk][:, j * 128:(j + 1) * 128], u.ap[:, kc * 128:(kc + 1) * 128], identB.ap),
                          reads=[u.buf], writes=[psb[pk]])
                dst = uT.ap[:, half * 8:(half + 1) * 8, tt * 128:(tt + 1) * 128]
                src = ps16_t[pk][:, 0:1024].rearrange("p (a b) -> p a b", b=128)
                if half == 0:
                    kb.op("act", ACP(dst, src), reads=[psb[pk]], writes=[uTb[tt]])
                else:
                    kb.op("dve", CP(dst, src), reads=[psb[pk]], writes=[uTb[tt]])

    def front_end_tile(*a):
        fe_norm(*a)
        fe_transpose(*a)

    def front_end(hin, blk, gbcs, uTs, hx_slots, u_slots, junk, small):
        for tt in range(4):
            front_end_tile(hin, blk, tt, gbcs, uTs, hx_slots, u_slots, junk, small)

    def make_uT(name):
        t = ar.tile(BF16, [16, 512], name)
        return (t, [kb.buf(name + "_%d" % i) for i in range(4)])

    def phase_proj(hin, gains, head_specs, v_spec, fz):
        ar.reset()
        ws = WStream()
        gbcs = []
        for gi, g in enumerate(gains):
            t = ar.tile(F32, [D], "gbc%d" % gi)
            kb.op("sp", DMA(t.ap, bcast_row(g, D)), writes=[t.buf], dma=True)
            gbcs.append(t)
        gcols = []
        for hi, hs in enumerate(head_specs):
            t = ar.tile(F32, [1], "gcol%d" % hi)
            kb.op("sp", DMA(t.ap, col_ap(hs[4])), writes=[t.buf], dma=True)
            if hs[5] != 1.0:
                t2 = ar.tile(F32, [1], "gcols%d" % hi)
                kb.op("dve", TS(t2.ap, t.ap, float(hs[5]), ALU.mult), reads=[t.buf], writes=[t2.buf])
                t = t2
            gcols.append(t)
        hx_slots = [ar.tile(F32, [D], "hx%d" % i) for i in range(2)]
        u_slots = [ar.tile(BF16, [D], "u%d" % i) for i in range(2 * len(gains))]
        junk = ar.tile(BF16, [D], "junk")
        small = [tuple(ar.tile(F32, [1], "sm%d_%d" % (i, j)) for j in range(3)) for i in range(2)]
        uT_sl = [[make_uT("uT%d_%d" % (gi, s)) for gi in range(len(gains))] for s in range(2)]
        sq_sl = [ar.tile(BF16, [512], "sq%d" % i) for i in range(2)]
        lnv_sl = [ar.tile(F32, [512], "lnv%d" % i) for i in range(2)]
        rs_sl = [ar.tile(F32, [512], "rs%d" % i) for i in range(2)]
        qn_sl = [ar.tile(BF16, [512], "qn%d" % i) for i in range(3)]
        vst_sl = [ar.tile(BF16, [512], "vst%d" % i) for i in range(3)]
        if fz:
            wf_t = ar.tile(BF16, [16, H], "wf")
            kb.op("sp", DMA(wf_t.ap, wf_b.rearrange("p (kc n) -> p kc n", n=H)), reads=[wbufs["wf"]], writes=[wf_t.buf], dma=True)
            bfb = ar.tile(F32, [H], "bfb")
            kb.op("sp", DMA(bfb.ap, bcast_row(a_b_f, H)), writes=[bfb.buf], dma=True)
            triF = ar.tile(F32, [128], "triF")
            kb.op("sp", DMA(triF.ap, c_tri), writes=[triF.buf], dma=True)
            nsp_all = ar.tile(F32, [NT, H], "nsp_all")
            nsp_bufs = [kb.buf("nsp%d" % i) for i in range(NT)]
            z_sl = [ar.tile(F32, [H], "z%d" % i) for i in range(2)]
            e_sl = [ar.tile(F32, [H], "e%d" % i) for i in range(2)]
            c_sl = [ar.tile(F32, [H], "c%d" % i) for i in range(2)]
            cT_sl = [ar.tile(F32, [128], "cT%d" % i) for i in range(2)]
        cnt = {"h": 0, "v": 0, "f": 0}
        HB = [2, 3, 4]
        SB = [5, 6]
        front_end(hin, 0, gbcs, uT_sl[0], hx_slots, u_slots, junk, small)
        for blk in range(NB):
            uTs = uT_sl[blk % 2]
            pend = None
            items = []
            for hi, hs in enumerate(head_specs):
                for cbl in range(4):
                    items.append((hi, cbl))
            for idx, (hi, cbl) in enumerate(items):
                if blk + 1 < NB:
                    fe_args = (hin, blk + 1, idx // 2, gbcs, uT_sl[(blk + 1) % 2], hx_slots, u_slots, junk, small)
                    if idx % 2 == 0:
                        fe_norm(*fe_args)
                    else:
                        fe_transpose(*fe_args)
                wdram, wbuf, cb0, gi, _, _, outd = head_specs[hi]
                ncb = wdram.shape[0] // 2
                uT, uTb = uTs[gi]
                wt = [ws.load(wdram, kg * ncb + cb0 + cbl, wbuf) for kg in range(2)]
                for hh in range(4):
                    head = cbl * 4 + hh
                    i = cnt["h"]
                    cnt["h"] += 1
                    pb = HB[i % 3]
                    for kc in range(16):
                        kb.op("pe", MM(ps_t[pb][:, :], wt[kc // 8].ap[:, kc % 8, hh * 128:(hh + 1) * 128], uT.ap[:, kc, :],
                                       kc == 0, kc == 15),
                              reads=[wt[kc // 8].buf] + uTb, writes=[psb[pb]])
                    sq = sq_sl[i % 2]
                    kb.op("act", ACT(sq.ap, ps_t[pb][:, :], AF.Square), reads=[psb[pb]], writes=[sq.buf])
                    if pend is not None:
                        pend()
                    def fin(i=i, pb=pb, sq=sq, hi=hi, head=head, outd=outd, blk=blk):
                        sb = SB[i % 2]
                        kb.op("pe", MM(ps_t[sb][:, :], onesB.ap, sq.ap, True, True), reads=[sq.buf], writes=[psb[sb]])
                        lnv = lnv_sl[i % 2]
                        rs = rs_sl[i % 2]
                        kb.op("act", ACT(lnv.ap, ps_t[sb][:, :], AF.Ln, scale=1.0 / DH, bias=EPS), reads=[psb[sb]], writes=[lnv.buf])
                        kb.op("act", ACT(rs.ap, lnv.ap, AF.Exp, scale=-0.5), reads=[lnv.buf], writes=[rs.buf])
                        qn = qn_sl[i % 3]
                        kb.op("dve", STT(qn.ap, ps_t[pb][:, :], gcols[hi].ap[:, 0:1], rs.ap, ALU.mult, ALU.mult),
                              reads=[psb[pb], rs.buf, gcols[hi].buf], writes=[qn.buf])
                        kb.op("pool", DMA(outd[head, :, blk * TB:(blk + 1) * TB], qn.ap), reads=[qn.buf], dma=True, sembuf=qn.buf)
                    pend = fin
            if pend is not None:
                pend()
                pend = None
            if v_spec is not None:
                wdram, wbuf, cb0, gi = v_spec
                ncb = wdram.shape[0] // 2
                uT, uTb = uTs[gi]
                for cbl in range(4):
                    wt = [ws.load(wdram, kg * ncb + cb0 + cbl, wbuf) for kg in range(2)]
                    for tt in range(4):
                        i = cnt["v"]
                        cnt["v"] += 1
                        pb = HB[(cnt["h"] + i) % 3]
                        for kc in range(16):
                            kb.op("pe", MM(ps_t[pb][:, :], uT.ap[:, kc, tt * 128:(tt + 1) * 128], wt[kc // 8].ap[:, kc % 8, :],
                                           kc == 0, kc == 15),
                                  reads=[wt[kc // 8].buf, uTb[tt]], writes=[psb[pb]])
                        vst = vst_sl[i % 3]
                        kb.op("dve", CP(vst.ap, ps_t[pb][:, :]), reads=[psb[pb]], writes=[vst.buf])
                        r0 = blk * TB + tt * 128
                        gt_ = blk * 4 + tt
                        v_dst = bass.AP(tensor=vv.tensor, offset=vv.offset + (cbl * 4) * 128 * NT * 128 + gt_ * 128,
                                        ap=[[NT * 128, 128], [128 * NT * 128, 4], [1, 128]])
                        kb.op("pool", DMA(v_dst, vst.ap.rearrange("p (h d) -> p h d", d=128)), reads=[vst.buf], dma=True, sembuf=vst.buf)
            if fz:
                uT, uTb = uTs[0]
                for tt in range(4):
                    gt = blk * 4 + tt
                    i = cnt["f"]
                    cnt["f"] += 1
                    pb = 7
                    for kc in range(16):
                        kb.op("pe", MM(ps_t[pb][:, 0:H], uT.ap[:, kc, tt * 128:(tt + 1) * 128], wf_t.ap[:, kc, :], kc == 0, kc == 15),
                              reads=[wf_t.buf, uTb[tt]], writes=[psb[pb]])
                    z = z_sl[i % 2]
                    e_ = e_sl[i % 2]
                    kb.op("dve", TT(z.ap, ps_t[pb][:, 0:H], bfb.ap, ALU.add), reads=[psb[pb], bfb.buf], writes=[z.buf])
                    kb.op("act", ACT(e_.ap, z.ap, AF.Exp, scale=-1.0), reads=[z.buf], writes=[e_.buf])
                    kb.op("act", ACT(z.ap, e_.ap, AF.Ln, bias=1.0), reads=[e_.buf], writes=[z.buf])
                    kb.op("dve", TS(nsp_all.ap[:, gt, :], z.ap, -1.0, ALU.mult), reads=[z.buf], writes=[nsp_bufs[gt]])
                    for j in range(gt + 1):
                        lhs = triF.ap if j == gt else onesF.ap
                        kb.op("pe", MM(ps_t[pb][:, 32:32 + H], lhs, nsp_all.ap[:, j, :], j == 0, j == gt),
                              reads=[nsp_bufs[j], triF.buf], writes=[psb[pb]])
                    c_ = c_sl[i % 2]
                    kb.op("dve", CP(c_.ap, ps_t[pb][:, 32:32 + H]), reads=[psb[pb]], writes=[c_.buf])
                    kb.op("pool", DMA(cdr[gt * 128:(gt + 1) * 128, :], c_.ap), reads=[c_.buf], dma=True, sembuf=c_.buf)
                    cT_ = cT_sl[i % 2]
                    kb.op("pe", TR(ps_t[pb][0:H, 64:192], c_.ap, identF.ap), reads=[c_.buf], writes=[psb[pb]])
                    kb.op("dve", CP(cT_.ap[0:H, :], ps_t[pb][0:H, 64:192]), reads=[psb[pb]], writes=[cT_.buf])
                    kb.op("pool", DMA(cTd[:, gt * 128:(gt + 1) * 128], cT_.ap[0:H, :]), reads=[cT_.buf], dma=True, sembuf=cT_.buf)
        kb.barrier()

    def phase_attn(fox):
        ar.reset()
        q_sl = [ar.tile(BF16, [S], "qh%d" % i) for i in range(2)]
        k_sl = [ar.tile(BF16, [S], "kh%d" % i) for i in range(2)]
        v_sl = [ar.tile(BF16, [NT, 128], "vh%d" % i) for i in range(2)]
        t_sl = [ar.tile(F32, [512], "t%d" % i) for i in range(6)]
        p_sl = [ar.tile(BF16, [512], "p%d" % i) for i in range(8)]
        ld_sl = [ar.tile(F32, [512], "ld%d" % i) for i in range(2)]
        SK = 5
        rd_sl = [ar.tile(F32, [512], "rd%d" % i) for i in range(2)]
        o_sl = [ar.tile(BF16, [512], "o%d" % i) for i in range(4)]
        if fox:
            c_all = ar.tile(F32, [NT, H], "c_all")
            nc_all = ar.tile(F32, [NT, H], "nc_all")
            kb.op("sp", DMA(c_all.ap, cdr.rearrange("(t p) h -> p t h", p=128)), writes=[c_all.buf], dma=True)
            kb.op("dve", TS(nc_all.ap, c_all.ap, -1.0, ALU.mult), reads=[c_all.buf], writes=[nc_all.buf])
            trim = ar.tile(F32, [128], "trim")
            kb.op("sp", DMA(trim.ap, c_trimask), writes=[trim.buf], dma=True)
            trimB = ar.tile(BF16, [128], "trimB")
            kb.op("dve", CP(trimB.ap, trim.ap), reads=[trim.buf], writes=[trimB.buf])
            cq_sl = [ar.tile(F32, [S], "cq%d" % i) for i in range(2)]
        else:
            bm_sl = [ar.tile(F32, [8, 512], "bm%d" % i) for i in range(2)]
        STB = [0, 1, 2]
        OB = [3, 4]
        DB = [5, 6]

        prep_q = []

        def head_prep(h):
            s = h % 2
            kb.op("sp", DMA(q_sl[s].ap, qT[h]), writes=[q_sl[s].buf], dma=True)
            kb.op("sp", DMA(k_sl[s].ap, kT[h]), writes=[k_sl[s].buf], dma=True)
            kb.op("sp", DMA(v_sl[s].ap, vv[h].rearrange("p (t d) -> p t d", d=128)),
                  writes=[v_sl[s].buf], dma=True)
            if fox:
                cq = cq_sl[s]
                kb.op("sp", DMA(cq.ap, bass.AP(tensor=cTd.tensor, offset=cTd.offset + h * S, ap=[[0, 128], [1, S]])),
                      writes=[cq.buf], dma=True)
            else:
                kb.op("sp", DMA(bm_sl[s].ap, BMx[h].rearrange("p (m q) -> p m q", q=512)), writes=[bm_sl[s].buf], dma=True)

        tiles = []
        for h in range(H):
            for qb in range(NB):
                if fox:
                    kts = list(range(0, 4 * qb + 4))
                else:
                    kts = list(range(max(0, 4 * qb - 4), 4 * qb + 4))
                for n, kt in enumerate(kts):
                    tiles.append((h, qb, kt, n == 0, n == len(kts) - 1))
        NTL = len(tiles)
        qcount = [0]

        B2R = [(0, 128), (0, 256), (0, 384), (0, 512), (0, 512), (128, 512), (256, 512), (384, 512)]

        def cols(qb, kt):
            if fox:
                return ((kt - 4 * qb) * 128 if kt >= 4 * qb else 0), 512
            return B2R[kt - (4 * qb - 4)]

        def stageABC(i):
            h, qb, kt, first, last = tiles[i]
            s = h % 2
            c0, c1 = cols(qb, kt)
            sb = STB[i % 3]
            q0 = qb * TB
            diag = fox and kt >= 4 * qb
            kb.op("pe", MM(ps_t[sb][:, c0:c1], k_sl[s].ap[:, kt * 128:(kt + 1) * 128], q_sl[s].ap[:, q0 + c0:q0 + c1], True, not diag),
                  reads=[k_sl[s].buf, q_sl[s].buf], writes=[psb[sb]])
            if diag:
                kb.op("pe", MM(ps_t[sb][:, c0:c0 + 128], identB.ap, trimB.ap, False, True), reads=[trimB.buf], writes=[psb[sb]])
            t = t_sl[i % 6]
            p = p_sl[i % 8]
            if fox:
                kb.op("dve", STT(t.ap[:, c0:512], ps_t[sb][:, c0:512], nc_all.ap[:, kt, h:h + 1], cq_sl[s].ap[:, q0 + c0:q0 + 512], ALU.add, ALU.add),
                      reads=[psb[sb], cq_sl[s].buf, nc_all.buf], writes=[t.buf])
                kb.op("act", ACT(p.ap[:, c0:512], t.ap[:, c0:512], AF.Exp), reads=[t.buf], writes=[p.buf])
            else:
                mp = 7 - (kt - (4 * qb - 4))
                kb.op("dve", TT(t.ap[:, c0:c1], ps_t[sb][:, c0:c1], bm_sl[s].ap[:, mp, c0:c1], ALU.add), reads=[psb[sb], bm_sl[s].buf], writes=[t.buf])
                kb.op("act", ACT(p.ap[:, c0:c1], t.ap[:, c0:c1], AF.Exp), reads=[t.buf], writes=[p.buf])

        def stageD(i):
            h, qb, kt, first, last = tiles[i]
            s = h % 2
            c0, c1 = cols(qb, kt)
            p = p_sl[i % 8]
            qi = qcount[0]
            ob = OB[qi % 2]
            db = DB[qi % 2]
            kb.op("pe", MM(ps_t[ob][:, c0:c1], v_sl[s].ap[:, kt, :], p.ap[:, c0:c1], first, last, skip=True),
                  reads=[v_sl[s].buf, p.buf], writes=[psb[ob]])
            kb.op("pe", MM(ps_t[db][:, c0:c1], onesB.ap, p.ap[:, c0:c1], first, last, skip=True),
                  reads=[p.buf], writes=[psb[db]])
            if last:
                rd = rd_sl[qi % 2]
                o = o_sl[qi % 4]
                ld = ld_sl[qi % 2]

                def ep_act(ld=ld, rd=rd, db=db):
                    kb.op("act", ACT(ld.ap, ps_t[db][:, :], AF.Ln), reads=[psb[db]], writes=[ld.buf])
                    kb.op("act", ACT(rd.ap, ld.ap, AF.Exp, scale=-1.0), reads=[ld.buf], writes=[rd.buf])

                def ep_dve(o=o, rd=rd, ob=ob, h=h, qb=qb):
                    kb.op("dve", TT(o.ap, ps_t[ob][:, :], rd.ap, ALU.mult), reads=[psb[ob], rd.buf], writes=[o.buf])
                    kb.op("sp", DMA(oT[h * 128:(h + 1) * 128, qb * TB:(qb + 1) * TB], o.ap), reads=[o.buf], dma=True, sembuf=o.buf)
                deferred.append((cur_iter[0] + 1, ep_act))
                deferred.append((cur_iter[0] + 3, ep_dve))
                qcount[0] += 1
                if fox:
                    if qb >= 1:
                        pump(1)
                elif qcount[0] % 3 == 0:
                    pump(1)

        head_first = {}
        for i, tl in enumerate(tiles):
            head_first.setdefault(tl[0], i)
        head_prep(0)
        while prep_q:
            prep_q.pop(0)()
        deferred = []
        cur_iter = [0]

        def run_deferred(upto):
            while deferred and deferred[0][0] <= upto:
                deferred.pop(0)[1]()

        for i in range(NTL + SK):
            cur_iter[0] = i
            if i < NTL:
                hh_ = tiles[i][0]
                if i == head_first[hh_]:
                    while prep_q:
                        prep_q.pop(0)()
                if i == head_first[hh_] + SK and hh_ + 1 < H:
                    head_prep(hh_ + 1)
                if prep_q:
                    prep_q.pop(0)()
                stageABC(i)
            run_deferred(i)
            if i >= SK:
                stageD(i - SK)
        run_deferred(10 ** 9)
        kb.barrier()

    def phase_outproj(wdram, wbuf, hin, hout):
        ar.reset()
        wres = []
        for kg in range(2):
            for cb in range(4):
                t = ar.tile(BF16, [8, 512], "wo%d_%d" % (kg, cb))
                kb.op("sp", DMA(t.ap, wdram[kg * 4 + cb].rearrange("p (kc n) -> p kc n", n=512)), reads=[wbuf[kg * 4 + cb]], writes=[t.buf], dma=True)
                wres.append(t)
        o_sl = [ar.tile(BF16, [16, 512], "ot%d" % i) for i in range(3)]
        hx_sl = [ar.tile(F32, [D], "hx%d" % i) for i in range(4)]
        PB = [0, 1, 2, 3]
        n = 0
        for tt in range(NT):
            otb = o_sl[(tt // 4) % 3]
            hx = hx_sl[tt % 4]
            if tt % 4 == 0:
                kb.op("sp", DMA(otb.ap, oT[:, tt * 128:(tt + 4) * 128].rearrange("(h p) t -> p h t", p=128)), writes=[otb.buf], dma=True)
            kb.op("sp", DMA(hx.ap, hin[tt * 128:(tt + 1) * 128, :]), writes=[hx.buf], dma=True)

            class _V:
                pass
            ot = _V()
            ot.ap = otb.ap[:, :, (tt % 4) * 128:(tt % 4 + 1) * 128]
            ot.buf = otb.buf
            for cb in range(4):
                pb = PB[n % 4]
                n += 1
                for hh in range(16):
                    w = wres[(hh // 8) * 4 + cb]
                    kb.op("pe", MM(ps_t[pb][:, :], ot.ap[:, hh, :], w.ap[:, hh % 8, :], hh == 0, hh == 15),
                          reads=[ot.buf, w.buf], writes=[psb[pb]])
                kb.op("dve", TT(hx.ap[:, cb * 512:(cb + 1) * 512], hx.ap[:, cb * 512:(cb + 1) * 512], ps_t[pb][:, :], ALU.add),
                      reads=[psb[pb], hx.buf], writes=[hx.buf])
            kb.op("pool", DMA(hout[tt * 128:(tt + 1) * 128, :], hx.ap), reads=[hx.buf], dma=True, sembuf=hx.buf)
        kb.barrier()

    def phase_mlp(l, hin, hout):
        ar.reset()
        ws = WStream()
        gbc = ar.tile(F32, [D], "gbc")
        kb.op("sp", DMA(gbc.ap, bcast_row(mlp_norm_g[l], D)), writes=[gbc.buf], dma=True)
        hxk = [ar.tile(F32, [D], "hxk%d" % i) for i in range(4)]
        fe_hx = [ar.tile(F32, [D], "fehx%d" % i) for i in range(2)]
        u_slots = [ar.tile(BF16, [D], "u%d" % i) for i in range(2)]
        junk = ar.tile(BF16, [D], "junk")
        small = [tuple(ar.tile(F32, [1], "sm%d_%d" % (i, j)) for j in range(3)) for i in range(2)]
        uTs = [make_uT("uT")]
        aT = ar.tile(BF16, [64, 512], "aT")
        aTb = [kb.buf("aT%d" % i) for i in range(64)]
        r_sl = [ar.tile(F32, [512], "r%d" % i) for i in range(3)]
        w1d, w2d = w1_b[l], w2_b[l]
        wb1, wb2 = wbufs["w1_%d" % l], wbufs["w2_%d" % l]
        P1 = [2, 3]
        P6 = [2, 3, 4, 5, 6, 7]
        n1 = 0
        front_end(hin, 0, [gbc], uTs, fe_hx, u_slots, junk, small)
        for blk in range(NB):
            uT, uTb = uTs[0]
            for cb in range(16):
                wt = [ws.load(w1d, kg * 16 + cb, wb1) for kg in range(2)]
                for ff in range(4):
                    fc = cb * 4 + ff
                    pb = P1[n1 % 2]
                    r = r_sl[n1 % 3]
                    n1 += 1
                    for kc in range(16):
                        kb.op("pe", MM(ps_t[pb][:, :], wt[kc // 8].ap[:, kc % 8, ff * 128:(ff + 1) * 128], uT.ap[:, kc, :], kc == 0, kc == 15),
                              reads=[wt[kc // 8].buf] + uTb, writes=[psb[pb]])
                    kb.op("act", ACT(r.ap, ps_t[pb][:, :], AF.Relu), reads=[psb[pb]], writes=[r.buf])
                    kb.op("dve", TT(aT.ap[:, fc, :], r.ap, r.ap, ALU.mult), reads=[r.buf], writes=[aTb[fc]])
            for tt in range(4):
                r0 = blk * TB + tt * 128
                kb.op("sp", DMA(hxk[tt].ap, hin[r0:r0 + 128, :]), writes=[hxk[tt].buf], dma=True)
            for cb in range(4):
                fe_args = (hin, blk + 1, cb, [gbc], uTs, fe_hx, u_slots, junk, small)
                if l == 0:
                    pump(1)
                if blk + 1 < NB:
                    fe_norm(*fe_args)
                P2 = [P6[(cb * 4 + tt) % 6] for tt in range(4)]
                for kg in range(8):
                    wt = ws.load(w2d, kg * 4 + cb, wb2)
                    if kg == 4 and blk + 1 < NB:
                        fe_transpose(*fe_args)
                    for fl in range(8):
                        fc = kg * 8 + fl
                        for tt in range(4):
                            kb.op("pe", MM(ps_t[P2[tt]][:, :], aT.ap[:, fc, tt * 128:(tt + 1) * 128], wt.ap[:, fl, :], fc == 0, fc == 63),
                                  reads=[aTb[fc], wt.buf], writes=[psb[P2[tt]]])
                for tt in range(4):
                    hx = hxk[tt]
                    kb.op("dve", TT(hx.ap[:, cb * 512:(cb + 1) * 512], hx.ap[:, cb * 512:(cb + 1) * 512], ps_t[P2[tt]][:, :], ALU.add),
                          reads=[psb[P2[tt]], hx.buf], writes=[hx.buf])
            for tt in range(4):
                r0 = blk * TB + tt * 128
                kb.op("pool", DMA(hout[r0:r0 + 128, :], hxk[tt].ap), reads=[hxk[tt].buf], dma=True, sembuf=hxk[tt].buf)
        kb.barrier()

    SC = float(DH) ** -0.5
    wb = wbufs
    def P1():
        phase_proj(x, [a_norm_g],
                   [(w_in_b, wb["w_in"], 0, 0, a_q_g, SC, qT), (w_in_b, wb["w_in"], 4, 0, a_k_g, 1.0, kT)],
                   (w_in_b, wb["w_in"], 8, 0), True)

    def P3():
        flush_casts("w_outA")
        phase_outproj(w_outA_b, wb["w_outA"], x, h1)

    def P4():
        flush_casts("w2_0")
        phase_mlp(0, h1, h2)

    def P5():
        flush_casts("b_wq")
        phase_proj(h2, [kv_norm_g, b_norm_g],
                   [(kv_w_b, wb["kv_w"], 0, 0, kv_k_g, 1.0, kT), (b_wq_b, wb["b_wq"], 0, 1, b_q_g, SC, qT)],
                   (kv_w_b, wb["kv_w"], 4, 0), False)

    def P7():
        flush_casts("w_outB")
        phase_outproj(w_outB_b, wb["w_outB"], h2, h3)

    def P8():
        flush_casts("w2_1")
        phase_mlp(1, h3, y)

    phases = [P1, lambda: phase_attn(True), P3, P4, P5, lambda: phase_attn(False), P7, P8]
    for ph in phases[:nph]:
        ph()

    kb.finalize(nc, stack)
    with nc.Block() as block:
        block.tensor(lambda e: kb.replay("pe", e))
        block.scalar(lambda e: kb.replay("act", e))
        block.vector(lambda e: kb.replay("dve", e))
        block.gpsimd(lambda e: kb.replay("pool", e))
        block.sync(lambda e: kb.replay("sp", e))
    stack.close()
    return nc


def make_consts(b_rel0):
    ident = np.eye(128, dtype=np.float32)
    tp = np.arange(128)
    tri = (tp[:, None] <= tp[None, :]).astype(np.float32)
    trimask = np.where(tp[None, :] >= tp[:, None], 0.0, NEG).astype(np.float32)
    kl = np.arange(128)[:, None, None]
    mp = np.arange(8)[None, :, None]
    ql = np.arange(512)[None, None, :]
    m = 7 - mp
    dc = 8 - 2 * m + ql // 64 - kl // 64
    valid = (dc >= 0) & (dc <= 8)
    dist = 512 - 128 * m + ql - kl
    idx = np.clip(dist, -63, 256) + 63
    idx = np.where(valid, idx, b_rel0.shape[1])
    table = np.concatenate([b_rel0, np.full((b_rel0.shape[0], 1), NEG, np.float32)], axis=1)
    BMx = np.ascontiguousarray(table[:, idx]).astype(np.float32).reshape(b_rel0.shape[0], 128, 8 * 512)
    return {"c_ident": ident, "c_tri": tri, "c_trimask": trimask, "BMx": BMx}


def make_in_map(xb, inputs, consts):
    m = {
        "x": np.ascontiguousarray(xb),
        "a_norm_g": inputs["a_norm_g"][0], "a_w_in": inputs["a_w_in"][0], "a_b_f": inputs["a_b_f"][0],
        "a_q_g": inputs["a_q_g"][0], "a_k_g": inputs["a_k_g"][0], "a_w_out": inputs["a_w_out"][0],
        "mlp_norm_g": inputs["mlp_norm_g"], "mlp_w1": inputs["mlp_w1"], "mlp_w2": inputs["mlp_w2"],
        "kv_norm_g": inputs["kv_norm_g"], "kv_w": inputs["kv_w"], "kv_k_g": inputs["kv_k_g"],
        "b_norm_g": inputs["b_norm_g"][0], "b_w_q": inputs["b_w_q"][0], "b_q_g": inputs["b_q_g"][0],
        "b_w_out": inputs["b_w_out"][0],
    }
    m.update(consts)
    return {k: np.ascontiguousarray(np.asarray(v, dtype=np.float32)) for k, v in m.items()}


_NC_CACHE = {}


def kernel(**inputs):
    inputs = {k: np.asarray(v) for k, v in inputs.items()}
    x = inputs["x"]
    B, S, _ = x.shape
    if S not in _NC_CACHE:
        _NC_CACHE[S] = build(S)
    nc = _NC_CACHE[S]
    consts = make_consts(np.asarray(inputs["b_rel"][0], dtype=np.float32))
    in_maps = [make_in_map(x[b], inputs, consts) for b in range(B)]
    res = run_bass_kernel_spmd(nc, in_maps, core_ids=list(range(B)))
    return np.stack([r["y"] for r in res.results], axis=0).astype(np.float32)
```
